# Optimizing a Trainium2 kernel written in Bass

```python
import math
import jax, jax.numpy as jnp
from jax import lax
import numpy as np

D_MODEL = 2048
BATCH = 1
SEQ = 8192
DEPTH = 2
DEC_BATCH = 4
DEC_SEQ = 2048
PAST_LEN = 128

N_MIXERS = 2
N_LRU_LAYERS = (DEPTH + 1) // 2
N_ATTN_LAYERS = DEPTH // 2
D_RNN = D_MODEL
LRU_BLOCKS = 16
LRU_BW = D_RNN // LRU_BLOCKS
CONV_W = 4
LRU_C = 8.0
HEAD_DIM = 128
N_HEADS = D_MODEL // HEAD_DIM
N_KV = 4
GROUP = N_HEADS // N_KV
WINDOW = 128
BLOCK = 128
QKV_DIM = (N_HEADS + 2 * N_KV) * HEAD_DIM
NUM_BUCKETS = 32
MAX_DISTANCE = 128
D_FF = int(math.ceil(8 * D_MODEL / 3 / 256) * 256)
ALPHA = (2 * DEPTH) ** 0.25
BETA = (8 * DEPTH) ** -0.25
LN_EPS = 1e-5
NEG = -1e30

kernel_name = 'hybrid_rglru_swa_encoder'


def layer_norm(x, g, b):
    xf = x.astype(jnp.float32)
    mu = xf.mean(-1, keepdims=True)
    var = jnp.mean(jnp.square(xf - mu), -1, keepdims=True)
    y = (xf - mu) * lax.rsqrt(var + LN_EPS)
    return (y * g.astype(jnp.float32) + b.astype(jnp.float32)).astype(x.dtype)


def swiglu(x, w_gu, w_down):
    gu = x @ w_gu
    g, u = gu[..., :D_FF], gu[..., D_FF:]
    return (jax.nn.silu(g) * u) @ w_down


def centred_depthwise_conv(x, w, b):
    S = x.shape[1]
    left = CONV_W // 2
    xp = jnp.pad(x, ((0, 0), (left, CONV_W - 1 - left), (0, 0)))
    y = b
    for k in range(CONV_W):
        y = y + xp[:, k:k + S] * w[k]
    return y


def _lin_combine(c1, c2):
    a1, b1 = c1
    a2, b2 = c2
    return a1 * a2, a2 * b1 + b2


def linear_recurrence(a, b, reverse):
    return lax.associative_scan(_lin_combine, (a, b), reverse=reverse, axis=1)[1]


def rglru_block(x, w_in, conv_w, conv_b, gate_w, gate_b, lam, w_out):
    B, S, _ = x.shape
    proj = x @ w_in
    y_branch = jax.nn.gelu(proj[..., :D_RNN])
    xr = centred_depthwise_conv(proj[..., D_RNN:], conv_w, conv_b)
    xb = xr.reshape(B, S, LRU_BLOCKS, LRU_BW)
    g = jnp.einsum('bsni,egnio->egbsno', xb, gate_w) + gate_b[:, :, None, None]
    g = jax.nn.sigmoid(g.astype(jnp.float32)).reshape(2, 2, B, S, D_RNN)
    r, i = g[:, 0], g[:, 1]
    log_a = -LRU_C * r * jax.nn.softplus(-lam.astype(jnp.float32))[:, None, None, :]
    a = jnp.exp(log_a)
    u = jnp.sqrt(-jnp.expm1(2.0 * log_a)) * i * xr.astype(jnp.float32)[None]
    h = linear_recurrence(a[0], u[0], False) + linear_recurrence(a[1], u[1], True)
    return (y_branch * h.astype(x.dtype)) @ w_out


def t5_bucket(rel):
    nb = NUM_BUCKETS // 2
    ret = (rel > 0).astype(np.int32) * nb
    n = np.abs(rel)
    max_exact = nb // 2
    nn = np.maximum(n, 1).astype(np.float32)
    large = max_exact + (np.log(nn / max_exact) / math.log(MAX_DISTANCE / max_exact) * (nb - max_exact)).astype(np.int32)
    large = np.minimum(large, nb - 1)
    return (ret + np.where(n < max_exact, n, large)).astype(np.int32)


def band_structure(S):
    nblk = S // BLOCK
    q = np.arange(BLOCK)[:, None]
    c = np.arange(3 * BLOCK)[None, :]
    rel = c - BLOCK - q
    band = np.abs(rel) <= WINDOW
    key_abs = (np.arange(nblk)[:, None] - 1) * BLOCK + np.arange(3 * BLOCK)[None, :]
    key_ok = (key_abs >= 0) & (key_abs < S)
    mask = band[None] & key_ok[:, None, :]
    return mask, t5_bucket(rel)


def _key_windows(t, B, nblk):
    tp = jnp.pad(t, ((0, 0), (BLOCK, BLOCK), (0, 0), (0, 0)))
    tb = tp.reshape(B, nblk + 2, BLOCK, N_KV, HEAD_DIM)
    return jnp.concatenate([tb[:, :-2], tb[:, 1:-1], tb[:, 2:]], axis=2)


def windowed_gqa(x, w_qkv, sink, w_out, rel_bias):
    B, S, _ = x.shape
    nblk = S // BLOCK
    mask, bucket = band_structure(S)
    qkv = x @ w_qkv
    q = qkv[..., :N_HEADS * HEAD_DIM].reshape(B, nblk, BLOCK, N_KV, GROUP, HEAD_DIM) * (HEAD_DIM ** -0.5)
    k = qkv[..., N_HEADS * HEAD_DIM:(N_HEADS + N_KV) * HEAD_DIM].reshape(B, S, N_KV, HEAD_DIM)
    v = qkv[..., (N_HEADS + N_KV) * HEAD_DIM:].reshape(B, S, N_KV, HEAD_DIM)
    kw = _key_windows(k, B, nblk)
    vw = _key_windows(v, B, nblk)
    bias = jnp.transpose(rel_bias[bucket], (2, 0, 1)).reshape(N_KV, GROUP, BLOCK, 3 * BLOCK)
    logits = jnp.einsum('bnqkgd,bnckd->bnkgqc', q, kw).astype(jnp.float32) + bias.astype(jnp.float32)
    logits = jnp.where(mask[None, :, None, None], logits, NEG)
    s = sink.reshape(N_KV, GROUP).astype(jnp.float32)[None, None, :, :, None, None]
    m = jnp.maximum(logits.max(-1, keepdims=True), s)
    p = jnp.exp(logits - m)
    p = p / (p.sum(-1, keepdims=True) + jnp.exp(s - m))
    o = jnp.einsum('bnkgqc,bnckd->bnqkgd', p.astype(vw.dtype), vw).reshape(B, S, N_HEADS * HEAD_DIM)
    return o @ w_out


def trunk(x, lru_w_in, lru_conv_w, lru_conv_b, lru_gate_w, lru_gate_b, lru_lambda, lru_w_out,
          attn_w_qkv, attn_sink, attn_w_out, rel_bias, ffn_w_gu, ffn_w_down, ln_g, ln_b):
    for l in range(DEPTH):
        j = l // N_MIXERS
        if l % N_MIXERS == 0:
            h = rglru_block(x, lru_w_in[j], lru_conv_w[j], lru_conv_b[j], lru_gate_w[j],
                            lru_gate_b[j], lru_lambda[j], lru_w_out[j])
        else:
            h = windowed_gqa(x, attn_w_qkv[j], attn_sink[j], attn_w_out[j], rel_bias)
        x = layer_norm(ALPHA * x + h, ln_g[l, 0], ln_b[l, 0])
        x = layer_norm(ALPHA * x + swiglu(x, ffn_w_gu[l], ffn_w_down[l]), ln_g[l, 1], ln_b[l, 1])
    return x


def setup_inputs(seed: int = 0) -> dict:
    key = jax.random.key(seed)
    ks = jax.random.split(key, 20)
    nrm = jax.random.normal
    f32 = jnp.float32
    u = jax.random.uniform(ks[7], (N_LRU_LAYERS, 2, D_RNN), f32, 0.9, 0.999)
    s = u ** (1.0 / LRU_C)
    return {
        'x_prompt': nrm(ks[0], (BATCH, SEQ, D_MODEL), f32),
        'x_sample': nrm(ks[1], (DEC_BATCH, DEC_SEQ, D_MODEL), f32),
        'lru_w_in': nrm(ks[2], (N_LRU_LAYERS, D_MODEL, 2 * D_RNN), f32) * D_MODEL ** -0.5,
        'lru_conv_w': nrm(ks[3], (N_LRU_LAYERS, CONV_W, D_RNN), f32) * CONV_W ** -0.5,
        'lru_conv_b': nrm(ks[4], (N_LRU_LAYERS, D_RNN), f32) * 0.01,
        'lru_gate_w': nrm(ks[5], (N_LRU_LAYERS, 2, 2, LRU_BLOCKS, LRU_BW, LRU_BW), f32) * LRU_BW ** -0.5,
        'lru_gate_b': nrm(ks[6], (N_LRU_LAYERS, 2, 2, LRU_BLOCKS, LRU_BW), f32) * 0.01,
        'lru_lambda': jnp.log(s) - jnp.log1p(-s),
        'lru_w_out': nrm(ks[8], (N_LRU_LAYERS, D_RNN, D_MODEL), f32) * (D_RNN ** -0.5 * BETA),
        'attn_w_qkv': nrm(ks[9], (N_ATTN_LAYERS, D_MODEL, QKV_DIM), f32) * D_MODEL ** -0.5,
        'attn_sink': nrm(ks[10], (N_ATTN_LAYERS, N_HEADS), f32) * 0.5,
        'attn_w_out': nrm(ks[11], (N_ATTN_LAYERS, N_HEADS * HEAD_DIM, D_MODEL), f32) * ((N_HEADS * HEAD_DIM) ** -0.5 * BETA),
        'rel_bias': nrm(ks[12], (NUM_BUCKETS, N_HEADS), f32) * 0.2,
        'ffn_w_gu': nrm(ks[13], (DEPTH, D_MODEL, 2 * D_FF), f32) * D_MODEL ** -0.5,
        'ffn_w_down': nrm(ks[14], (DEPTH, D_FF, D_MODEL), f32) * (D_FF ** -0.5 * BETA),
        'ln_g': 1.0 + 0.02 * nrm(ks[15], (DEPTH, 2, D_MODEL), f32),
        'ln_b': 0.02 * nrm(ks[16], (DEPTH, 2, D_MODEL), f32),
    }


def reference(x_prompt, x_sample, lru_w_in, lru_conv_w, lru_conv_b, lru_gate_w, lru_gate_b, lru_lambda,
              lru_w_out, attn_w_qkv, attn_sink, attn_w_out, rel_bias, ffn_w_gu, ffn_w_down, ln_g, ln_b):
    y_prompt = trunk(x_prompt, lru_w_in, lru_conv_w, lru_conv_b, lru_gate_w, lru_gate_b, lru_lambda, lru_w_out,
                     attn_w_qkv, attn_sink, attn_w_out, rel_bias, ffn_w_gu, ffn_w_down, ln_g, ln_b)
    y_sample = trunk(x_sample, lru_w_in, lru_conv_w, lru_conv_b, lru_gate_w, lru_gate_b, lru_lambda, lru_w_out,
                     attn_w_qkv, attn_sink, attn_w_out, rel_bias, ffn_w_gu, ffn_w_down, ln_g, ln_b)
    return (y_prompt, y_sample)
```

```python
import math
from contextlib import ExitStack
import numpy as np
import concourse.bass as bass
import concourse.mybir as mybir
from concourse.bass_utils import run_bass_kernel_spmd

F32 = mybir.dt.float32
BF16 = mybir.dt.bfloat16
AF = mybir.ActivationFunctionType
ALU = mybir.AluOpType
AX = mybir.AxisListType

NCORES = 8
NT = 2048
NTH = NT + 128
D = 2048
KC = 16
DFF = 5632
JC = 44
ALPHA = 4.0 ** 0.25
LN_EPS = 1e-5
NEG = -1e30
GELU_C = 0.7978845608028654

V_CW = 0
V_CB = 64
V_GB = 80
V_LAM = 144
V_LNG = 176
V_LNB = 240
V_SINK = 304
V_SELF = 320
V_SELB = 328
V_HM = 336
V_SELL = 338
V_SELR = 346
NV = 360


class Res:
    __slots__ = ('w', 'r')

    def __init__(self):
        self.w = None
        self.r = []


class Prog:
    ENGS = ('pe', 'act', 'dve', 'pool', 'sp')

    def __init__(self, nc):
        self.nc = nc
        self.streams = {e: [] for e in self.ENGS}
        self.sems = {}
        self.cnt = {}
        for e in ('pe', 'act', 'dve', 'pool'):
            self.sems[e] = nc.alloc_semaphore('s_' + e)
            self.cnt[e] = 0
        self.seen = {e: {} for e in self.ENGS}

    def _waits(self, eng, reads, writes, extra=()):
        need = {}

        def add(tok):
            if tok is None:
                return
            k, v, te = tok
            if te == 'pe' and eng == 'pe':
                return
            if need.get(k, 0) < v:
                need[k] = v
        for r in reads:
            add(r.w)
        for w in writes:
            add(w.w)
            for t in w.r:
                add(t)
        for t in extra:
            add(t)
        out = []
        seen = self.seen[eng]
        for k, v in need.items():
            if seen.get(k, 0) >= v:
                continue
            seen[k] = v
            out.append((self.sems[k], v))
        return out

    def _commit(self, tok, reads, writes):
        for r in reads:
            r.r.append(tok)
        for w in writes:
            w.w = tok
            w.r = []

    def op(self, eng, fn, reads=(), writes=(), extra=()):
        waits = self._waits(eng, reads, writes, extra)
        self.cnt[eng] += 1
        tok = (eng, self.cnt[eng], eng)
        self.streams[eng].append((waits, fn, (self.sems[eng], 1)))
        self._commit(tok, reads, writes)
        return tok

    def dma(self, q, key, out, in_, reads=(), writes=(), extra=()):
        if key not in self.sems:
            self.sems[key] = self.nc.alloc_semaphore('d_' + str(len(self.sems)))
            self.cnt[key] = 0
        waits = self._waits(q, reads, writes, extra)
        self.cnt[key] += 16
        tok = (key, self.cnt[key], 'dma')
        self.streams[q].append((waits, lambda e: e.dma_start(out=out, in_=in_), (self.sems[key], 16)))
        self._commit(tok, reads, writes)
        return tok

    def coll(self, key, fn, reads=(), writes=()):
        if key not in self.sems:
            self.sems[key] = self.nc.alloc_semaphore('d_' + str(len(self.sems)))
            self.cnt[key] = 0
        waits = self._waits('pool', reads, writes)
        self.cnt[key] += 1
        tok = (key, self.cnt[key], 'dma')
        self.streams['pool'].append((waits, fn, (self.sems[key], None)))
        self._commit(tok, reads, writes)
        return tok

    def barrier(self):
        toks = [(k, v, 'x') for k, v in self.cnt.items() if v > 0]
        for e in self.ENGS:
            waits = self._waits(e, (), (), toks)
            if waits:
                self.streams[e].append((waits, None, None))

    def emit(self):
        nc = self.nc
        streams = self.streams
        self.streams = {e: [] for e in self.ENGS}

        def run(e, stream):
            for waits, fn, inc in stream:
                for s, v in waits:
                    e.wait_ge(s, v)
                if fn is not None:
                    ins = fn(e)
                    if inc[1] is None:
                        ins.then_inc(inc[0])
                    else:
                        ins.then_inc(inc[0], inc[1])
        with nc.Block() as block:
            @block.tensor
            def _(e):
                run(e, streams['pe'])

            @block.scalar
            def _(e):
                run(e, streams['act'])

            @block.vector
            def _(e):
                run(e, streams['dve'])

            @block.gpsimd
            def _(e):
                run(e, streams['pool'])

            @block.sync
            def _(e):
                run(e, streams['sp'])

    def mm(self, out, lhsT, rhs, start, stop, reads, writes):
        return self.op('pe', lambda e: e.matmul(out, lhsT, rhs, start=start, stop=stop), reads, writes)

    def tr(self, out, in_, ident, reads, writes):
        return self.op('pe', lambda e: e.transpose(out, in_, ident), reads, writes)

    def act(self, out, in_, func, reads, writes, bias=None, scale=None, accum_out=None):
        kw = {}
        if bias is not None:
            kw['bias'] = bias
        if scale is not None:
            kw['scale'] = scale
        if accum_out is not None:
            kw['accum_out'] = accum_out
        return self.op('act', lambda e: e.activation(out=out, in_=in_, func=func, **kw), reads, writes)

    def tt(self, eng, out, in0, in1, op, reads, writes):
        return self.op(eng, lambda e: e.tensor_tensor(out=out, in0=in0, in1=in1, op=op), reads, writes)

    def ts(self, eng, out, in0, s1, s2, op0, op1, reads, writes):
        if op1 is None:
            return self.op(eng, lambda e: e.tensor_scalar(out=out, in0=in0, scalar1=s1, scalar2=None, op0=op0), reads, writes)
        return self.op(eng, lambda e: e.tensor_scalar(out=out, in0=in0, scalar1=s1, scalar2=s2, op0=op0, op1=op1), reads, writes)

    def stt(self, eng, out, in0, scalar, in1, op0, op1, reads, writes):
        return self.op(eng, lambda e: e.scalar_tensor_tensor(out=out, in0=in0, scalar=scalar, in1=in1, op0=op0, op1=op1), reads, writes)

    def cp(self, eng, out, in_, reads, writes):
        return self.op(eng, lambda e: e.tensor_copy(out=out, in_=in_), reads, writes)


def build_program(stage=99, debug=False, T_FFN=512, T_LIN=1024):
    nc = bass.Bass("TRN2", target_bir_lowering=False)

    def din(name, shape, dt=F32):
        return nc.dram_tensor(name, list(shape), dt, kind="ExternalInput").ap()

    x_in = din("x", [NTH, D])
    vecs_in = din("vecs", [128, NV])
    ident_in = din("ident", [128, 128])
    bias_in = din("attn_bias", [128, 16, 384])
    band_in = din("band", [128, 384])
    w_in = din("lru_w_in", [D, 2 * D])
    gate_w = din("lru_gate_w", [2, 2, 16, 128, 128])
    w_lo = din("lru_w_out", [D, D])
    w_qkv = din("attn_w_qkv", [D, 3072])
    w_ao = din("attn_w_out", [D, D])
    w_gu = din("ffn_w_gu", [2, D, 2 * DFF])
    w_dn = din("ffn_w_down", [2, DFF, D])
    y_out = nc.dram_tensor("y", [NT, D], F32, kind="ExternalOutput").ap()

    dk = "Internal"
    X0 = nc.dram_tensor("X0", [128, KC, NTH], F32, kind="Internal").ap()
    X1 = nc.dram_tensor("X1", [128, KC, NT], F32, kind=dk).ap()
    X2 = nc.dram_tensor("X2", [128, KC, NT], F32, kind=dk).ap()
    X3 = nc.dram_tensor("X3", [128, KC, NT], F32, kind=dk).ap()
    GT = nc.dram_tensor("GT", [128, KC, NT], BF16, kind="Internal").ap()
    ag1_in = nc.dram_tensor("ag1_in", [128, 64], F32, kind="Internal").ap()
    ag1_out = nc.dram_tensor("ag1_out", [NCORES * 128, 64], F32, kind="Internal").ap()
    ag2_in = nc.dram_tensor("ag2_in", [128, KC * 256], F32, kind="Internal").ap()
    ag2_out = nc.dram_tensor("ag2_out", [NCORES * 128, KC * 256], F32, kind="Internal").ap()

    P = Prog(nc)
    rg = [list(range(NCORES))]
    uid = [0]

    def un(name):
        uid[0] += 1
        return f"{name}_{uid[0]}"

    vecs = nc.alloc_sbuf_tensor("vecs_sb", [128, NV], F32)
    ident = nc.alloc_sbuf_tensor("identf", [128, 128], F32)
    identb = nc.alloc_sbuf_tensor("identb", [128, 128], BF16)
    ones = nc.alloc_sbuf_tensor("ones", [128, 128], F32)
    cvec = nc.alloc_sbuf_tensor("cvec", [128, 64], F32)
    tmpv = nc.alloc_sbuf_tensor("tmpv", [128, 64], F32)
    carry = nc.alloc_sbuf_tensor("carry", [128, 32], F32)
    consts_r = Res()
    carry_r = Res()
    psA = nc.alloc_psum_tensor("psA", [128, 2048], F32)
    psB = nc.alloc_psum_tensor("psB", [128, 2048], F32)
    ps_r = [Res() for _ in range(8)]

    def bank(i):
        t = psA if i < 4 else psB
        return t[:, (i % 4) * 512:(i % 4 + 1) * 512]

    bank_ctr = [0]

    def next_bank():
        b = bank_ctr[0] % 8
        bank_ctr[0] += 1
        return b

    def vcol(c, n=1):
        return vecs[:, c:c + n]

    P.dma('sp', 'c0', vecs[:], vecs_in[:], writes=[consts_r])
    P.dma('sp', 'c0', ident[:], ident_in[:], writes=[consts_r])
    P.dma('pool', 'c1', identb[:], ident_in[:], writes=[consts_r])
    P.op('dve', lambda e: e.memset(ones[:], 1.0), writes=[consts_r])
    P.op('dve', lambda e: e.memset(carry[:], 0.0), writes=[carry_r])
    yv = tmpv[:, 0:32]
    acc = tmpv[:, 32:64]
    P.act(yv, vecs[:, V_LAM:V_LAM + 32], AF.Exp, [consts_r], [consts_r], scale=-1.0)
    P.ts('dve', acc, yv, -1.0 / 6.0, 1.0 / 5.0, ALU.mult, ALU.add, [consts_r], [consts_r])
    for cst in (1.0 / 4.0, 1.0 / 3.0, 1.0 / 2.0, 1.0):
        P.tt('dve', acc, acc, yv, ALU.mult, [consts_r], [consts_r])
        P.ts('dve', acc, acc, -1.0, cst, ALU.mult, ALU.add, [consts_r], [consts_r])
    P.tt('dve', acc, acc, yv, ALU.mult, [consts_r], [consts_r])
    P.ts('dve', cvec[:, 0:32], acc, -8.0, None, ALU.mult, None, [consts_r], [consts_r])
    P.ts('dve', cvec[:, 32:64], acc, -16.0, None, ALU.mult, None, [consts_r], [consts_r])
    P.barrier()
    P.emit()

    with ExitStack() as es:
        xs = [es.enter_context(nc.sbuf_tensor(un(f"xs{i}"), [128, D], F32)) for i in range(2)]
        xs_r = [Res() for _ in range(2)]
        xt = [es.enter_context(nc.sbuf_tensor(un(f"xt{i}"), [128, KC, 128], F32)) for i in range(2)]
        xt_r = [Res() for _ in range(2)]
        X0_r = Res()
        for b in range(NTH // 128):
            s = b % 2
            P.dma('sp', ('xs', s), xs[s][:], x_in[b * 128:(b + 1) * 128, :], writes=[xs_r[s]])
            for c4 in range(4):
                bk = next_bank()
                for j in range(4):
                    k = c4 * 4 + j
                    P.tr(bank(bk)[:, j * 128:(j + 1) * 128], xs[s][:, k * 128:(k + 1) * 128], ident[:],
                         [xs_r[s], consts_r], [ps_r[bk]])
                dst = xt[s][:, c4 * 4:(c4 + 1) * 4, :]
                src = bank(bk).rearrange("p (j t) -> p j t", j=4)
                if c4 % 2 == 0:
                    P.act(dst, src, AF.Copy, [ps_r[bk]], [xt_r[s]])
                else:
                    P.cp('dve', dst, src, [ps_r[bk]], [xt_r[s]])
            P.dma('sp', ('xt', s), X0[:, :, b * 128:(b + 1) * 128], xt[s][:], reads=[xt_r[s]])
        P.barrier()
        P.emit()

    with ExitStack() as es:
        def sb(name, shape, dt=F32):
            return es.enter_context(nc.sbuf_tensor(un(name), list(shape), dt))
        xin = sb("l_xin", [128, KC, NTH], BF16)
        xin_r = Res()
        gw = sb("l_gw", [128, 64, 128], BF16)
        gw_r = Res()
        NWS = 3
        wsl = [sb(f"l_w{i}", [128, KC, 128], BF16) for i in range(NWS)]
        wsl_r = [Res() for _ in range(NWS)]
        pext = sb("l_pext", [128, NT + 4])
        pext_r = Res()
        ph = sb("l_ph", [128, 4])
        xr = sb("l_xr", [128, NT]); xr_r = Res()
        xrb = sb("l_xrb", [128, NT], BF16); xrb_r = Res()
        yb = sb("l_y", [128, NT]); yb_r = Res()
        rr = [sb(f"l_r{e}", [128, NT]) for e in range(2)]; rr_r = [Res(), Res()]
        ii = [sb(f"l_i{e}", [128, NT]) for e in range(2)]; ii_r = [Res(), Res()]
        ss = [sb(f"l_s{e}", [128, NT]) for e in range(2)]; ss_r = [Res(), Res()]
        hh = [sb(f"l_h{e}", [128, NT]) for e in range(2)]; hh_r = [Res(), Res()]
        gts = [sb(f"l_gt{i}", [128, NT], BF16) for i in range(2)]; gts_r = [Res(), Res()]
        racc = sb("l_racc", [128, 8]); racc_r = Res()
        E = sb("l_E", [128, 64]); E_r = Res()
        Eg = sb("l_Eg", [128, NCORES, 64]); Eg_r = Res()
        ct = sb("l_ct", [128, 48]); ct_r = Res()

        for t in range(NTH // 512 + 1):
            lo = t * 512
            hi = min(lo + 512, NTH)
            P.dma('pool', 'lx', xin[:, :, lo:hi], X0[:, :, lo:hi], writes=[xin_r])
        P.dma('pool', 'lg', gw[:], gate_w.rearrange("e g n i o -> i (e g n) o"), writes=[gw_r])
        w_in_v = w_in.rearrange("(k p) m -> p k m", p=128)
        wit = [0]

        def load_w(col0):
            s = wit[0] % NWS
            wit[0] += 1
            P.dma('pool', ('lw', s), wsl[s][:], w_in_v[:, :, col0:col0 + 128], writes=[wsl_r[s]])
            return s

        GT_r = Res()
        for ps_ in (1, 2):
            for n in range(16):
                s = load_w(D + n * 128)
                for t in range(4):
                    bk = next_bank()
                    for k in range(KC):
                        P.mm(bank(bk), wsl[s][:, k, :], xin[:, k, t * 512:(t + 1) * 512], k == 0, k == KC - 1,
                             [wsl_r[s], xin_r], [ps_r[bk]])
                    P.act(pext[:, 2 + t * 512:2 + (t + 1) * 512], bank(bk), AF.Copy, [ps_r[bk]], [pext_r])
                bk = next_bank()
                for k in range(KC):
                    P.mm(bank(bk)[:, 0:4], wsl[s][:, k, :], xin[:, k, NT:NT + 4], k == 0, k == KC - 1,
                         [wsl_r[s], xin_r], [ps_r[bk]])
                P.act(pext[:, 0:2], bank(bk)[:, 0:2], AF.Copy, [ps_r[bk]], [pext_r])
                P.act(pext[:, NT + 2:NT + 3], bank(bk)[:, 2:3], AF.Copy, [ps_r[bk]], [pext_r])
                P.ts('dve', xr[:], pext[:, 2:NT + 2], vcol(V_CW + 32 + n), vcol(V_CB + n), ALU.mult, ALU.add,
                     [pext_r, consts_r], [xr_r])
                for tap, off in ((1, 1), (0, 0), (3, 3)):
                    P.stt('dve', xr[:], pext[:, off:off + NT], vcol(V_CW + tap * 16 + n), xr[:], ALU.mult, ALU.add,
                          [pext_r, consts_r, xr_r], [xr_r])
                P.cp('pool', xrb[:], xr[:], [xr_r], [xrb_r])
                if ps_ == 2:
                    s2 = load_w(n * 128)
                    for t in range(4):
                        bk = next_bank()
                        for k in range(KC):
                            P.mm(bank(bk), wsl[s2][:, k, :], xin[:, k, t * 512:(t + 1) * 512], k == 0, k == KC - 1,
                                 [wsl_r[s2], xin_r], [ps_r[bk]])
                        P.act(yb[:, t * 512:(t + 1) * 512], bank(bk), AF.Gelu_apprx_tanh, [ps_r[bk]], [yb_r])
                for e_ in range(2):
                    for g_ in range(2):
                        dstb = rr[e_] if g_ == 0 else ii[e_]
                        dst_r = rr_r[e_] if g_ == 0 else ii_r[e_]
                        for t in range(4):
                            bk = next_bank()
                            P.mm(bank(bk), gw[:, (e_ * 2 + g_) * 16 + n, :], xrb[:, t * 512:(t + 1) * 512], True, True,
                                 [gw_r, xrb_r], [ps_r[bk]])
                            if g_ == 0 and ps_ == 1:
                                P.act(dstb[:, t * 512:(t + 1) * 512], bank(bk), AF.Sigmoid, [ps_r[bk], consts_r],
                                      [dst_r, racc_r], bias=vcol(V_GB + (e_ * 2 + g_) * 16 + n),
                                      accum_out=racc[:, e_ * 4 + t:e_ * 4 + t + 1])
                            else:
                                P.act(dstb[:, t * 512:(t + 1) * 512], bank(bk), AF.Sigmoid, [ps_r[bk], consts_r],
                                      [dst_r], bias=vcol(V_GB + (e_ * 2 + g_) * 16 + n))
                    cc = cvec[:, e_ * 16 + n:e_ * 16 + n + 1]
                    cc2 = cvec[:, 32 + e_ * 16 + n:32 + e_ * 16 + n + 1]
                    P.act(ss[e_][:], rr[e_][:], AF.Exp, [rr_r[e_], consts_r], [ss_r[e_]], scale=cc2)
                    P.act(rr[e_][:], rr[e_][:], AF.Exp, [rr_r[e_], consts_r], [rr_r[e_]], scale=cc)
                    P.act(ss[e_][:], ss[e_][:], AF.Sqrt, [ss_r[e_], consts_r], [ss_r[e_]], scale=-1.0, bias=ones[:, 0:1])
                    P.tt('pool', ii[e_][:], ii[e_][:], ss[e_][:], ALU.mult, [ss_r[e_], ii_r[e_]], [ii_r[e_]])
                    P.tt('pool', ii[e_][:], ii[e_][:], xr[:], ALU.mult, [xr_r, ii_r[e_]], [ii_r[e_]])
                    init = 0.0 if ps_ == 1 else carry[:, e_ * 16 + n:e_ * 16 + n + 1]
                    if e_ == 0:
                        P.op('dve', lambda e, e_=e_, init=init: e.tensor_tensor_scan(
                            out=hh[e_][:], data0=rr[e_][:], data1=ii[e_][:], initial=init, op0=ALU.mult, op1=ALU.add),
                            [rr_r[e_], ii_r[e_], carry_r], [hh_r[e_]])
                    else:
                        P.op('dve', lambda e, e_=e_, init=init: e.tensor_tensor_scan(
                            out=hh[e_][:, ::-1], data0=rr[e_][:, ::-1], data1=ii[e_][:, ::-1], initial=init,
                            op0=ALU.mult, op1=ALU.add),
                            [rr_r[e_], ii_r[e_], carry_r], [hh_r[e_]])
                if ps_ == 1:
                    P.cp('dve', E[:, n:n + 1], hh[0][:, NT - 1:NT], [hh_r[0]], [E_r])
                    P.cp('dve', E[:, 32 + n:32 + n + 1], hh[1][:, 0:1], [hh_r[1]], [E_r])
                    for e_ in range(2):
                        P.op('dve', lambda e, e_=e_: e.tensor_reduce(out=ct[:, e_:e_ + 1], in_=racc[:, e_ * 4:e_ * 4 + 4],
                                                                     axis=AX.X, op=ALU.add), [racc_r], [ct_r])
                        P.act(E[:, 16 + 32 * e_ + n:16 + 32 * e_ + n + 1], ct[:, e_:e_ + 1], AF.Exp, [ct_r, consts_r], [E_r],
                              scale=cvec[:, e_ * 16 + n:e_ * 16 + n + 1])
                else:
                    g = n % 2
                    P.tt('dve', hh[0][:], hh[0][:], hh[1][:], ALU.add, [hh_r[0], hh_r[1]], [hh_r[0]])
                    P.tt('dve', gts[g][:], hh[0][:], yb[:], ALU.mult, [hh_r[0], yb_r], [gts_r[g]])
                    P.dma('sp', ('gt', g), GT[:, n, :], gts[g][:], reads=[gts_r[g]])
            if ps_ == 1:
                P.dma('sp', 'ag1', ag1_in[:], E[:], reads=[E_r], writes=[Eg_r])
                P.coll('ag1c', lambda e: e.collective_compute("AllGather", ALU.bypass, replica_groups=rg,
                                                              ins=[ag1_in.opt()], outs=[ag1_out.opt()]),
                       reads=[Eg_r], writes=[Eg_r])
                P.dma('sp', 'ag1', Eg[:], ag1_out.rearrange("(r p) c -> p r c", p=128), reads=[Eg_r], writes=[Eg_r])
                for dr in range(2):
                    order = range(NCORES) if dr == 0 else range(NCORES - 1, -1, -1)
                    cy = carry[:, dr * 16:(dr + 1) * 16]
                    for j in order:
                        sel = vcol((V_SELF if dr == 0 else V_SELB) + j)
                        hj = Eg[:, j, dr * 32:dr * 32 + 16]
                        Aj = Eg[:, j, dr * 32 + 16:dr * 32 + 32]
                        P.ts('dve', ct[:, 16:32], Aj, -1.0, sel, ALU.add, ALU.mult, [Eg_r, consts_r, ct_r], [ct_r])
                        P.ts('dve', ct[:, 16:32], ct[:, 16:32], 1.0, None, ALU.add, None, [ct_r], [ct_r])
                        P.ts('dve', ct[:, 32:48], hj, sel, None, ALU.mult, None, [Eg_r, consts_r, ct_r], [ct_r])
                        P.tt('dve', cy, cy, ct[:, 16:32], ALU.mult, [ct_r, carry_r], [carry_r])
                        P.tt('dve', cy, cy, ct[:, 32:48], ALU.add, [ct_r, carry_r], [carry_r])
        P.barrier()
        P.emit()

    def ln_apply(z, z_r, s1, s2, s3, st_r, T, lnidx, t0, Xout, Xout_r, final, ystage, ystage_r, ykey):
        for hcol in range(T // 512):
            sl = slice(hcol * 512, (hcol + 1) * 512)
            for src in (s1, s2):
                bk = next_bank()
                P.mm(bank(bk), ones[:], src[:, sl], True, True, [st_r, consts_r], [ps_r[bk]])
                P.act(src[:, sl], bank(bk), AF.Copy, [ps_r[bk]], [st_r], scale=1.0 / D)
        P.tt('dve', s3[:], s1[:], s1[:], ALU.mult, [st_r], [st_r])
        P.tt('dve', s2[:], s2[:], s3[:], ALU.subtract, [st_r], [st_r])
        P.ts('dve', s2[:], s2[:], LN_EPS, None, ALU.add, None, [st_r], [st_r])
        P.act(s2[:], s2[:], AF.Sqrt, [st_r], [st_r])
        P.op('dve', lambda e: e.reciprocal(out=s2[:], in_=s2[:]), [st_r], [st_r])
        for m in range(KC):
            zm = z[:, m, :]
            P.tt('dve', zm, zm, s1[:], ALU.subtract, [z_r[m], st_r], [z_r[m]])
            P.tt('dve', zm, zm, s2[:], ALU.mult, [z_r[m], st_r], [z_r[m]])
            P.act(zm, zm, AF.Identity, [z_r[m], consts_r], [z_r[m]],
                  bias=vcol(V_LNB + lnidx * 16 + m), scale=vcol(V_LNG + lnidx * 16 + m))
            if not final:
                P.dma('sp', ('xo', m), Xout[:, m, t0:t0 + T], zm, reads=[z_r[m]])
        if final:
            for tb in range(T // 128):
                ys = tb % 2
                for c4 in range(4):
                    bk = next_bank()
                    for j in range(4):
                        m = c4 * 4 + j
                        P.tr(bank(bk)[:, j * 128:(j + 1) * 128], z[:, m, tb * 128:(tb + 1) * 128], ident[:],
                             [z_r[m], consts_r], [ps_r[bk]])
                    dst = ystage[ys][:, c4 * 512:(c4 + 1) * 512]
                    if c4 % 2 == 0:
                        P.act(dst, bank(bk), AF.Copy, [ps_r[bk]], [ystage_r[ys]])
                    else:
                        P.cp('dve', dst, bank(bk), [ps_r[bk]], [ystage_r[ys]])
                P.dma('sp', (ykey, ys), Xout[t0 + tb * 128:t0 + (tb + 1) * 128, :], ystage[ys][:],
                      reads=[ystage_r[ys]])

    def z_chunk(z, z_r, m, sl, bk, xres, xres_r, s1, s2, sq, sq_r, st_r):
        zs = z[:, m, sl]
        P.stt('dve', zs, xres[:, sl], ALPHA, bank(bk), ALU.mult, ALU.add, [xres_r, ps_r[bk]], [z_r[m]])
        P.act(sq[:, sl], zs, AF.Square, [z_r[m]], [sq_r])
        if m == 0:
            P.cp('dve', s1[:, sl], zs, [z_r[m]], [st_r])
            P.cp('dve', s2[:, sl], sq[:, sl], [sq_r], [st_r])
        else:
            P.tt('dve', s1[:, sl], s1[:, sl], zs, ALU.add, [z_r[m], st_r], [st_r])
            P.tt('dve', s2[:, sl], s2[:, sl], sq[:, sl], ALU.add, [sq_r, st_r], [st_r])

    def linear_ln_phase(W, Xres, Xout, lnidx, T):
        with ExitStack() as es:
            def sb(name, shape, dt=F32):
                return es.enter_context(nc.sbuf_tensor(un(name), list(shape), dt))
            ain = [sb(f"o_ain{i}", [128, KC, T], BF16) for i in range(2)]
            ain_r = [Res(), Res()]
            z = sb("o_z", [128, KC, T])
            z_r = [Res() for _ in range(KC)]
            NWS = 3
            wsl = [sb(f"o_w{i}", [128, KC, 128], BF16) for i in range(NWS)]
            wsl_r = [Res() for _ in range(NWS)]
            xres = [sb(f"o_xr{i}", [128, T]) for i in range(2)]
            xres_r = [Res(), Res()]
            s1 = sb("o_s1", [128, T]); s2 = sb("o_s2", [128, T]); s3 = sb("o_s3", [128, T])
            sq = sb("o_sq", [128, T]); sq_r = Res()
            st_r = Res()
            Xout_r = Res()
            Wv = W.rearrange("(k p) m -> p k m", p=128)
            it = 0
            for ti in range(NT // T):
                t0 = ti * T
                a = ti % 2
                P.dma('sp', ('oa', a), ain[a][:], GT[:, :, t0:t0 + T], writes=[ain_r[a]])
                for m in range(KC):
                    s = it % NWS
                    P.dma('pool', ('ow', s), wsl[s][:], Wv[:, :, m * 128:(m + 1) * 128], writes=[wsl_r[s]])
                    xs_ = it % 2
                    P.dma('sp', ('ox', xs_), xres[xs_][:], Xres[:, m, t0:t0 + T], writes=[xres_r[xs_]])
                    it += 1
                    for hcol in range(T // 512):
                        sl = slice(hcol * 512, (hcol + 1) * 512)
                        bk = next_bank()
                        for k in range(KC):
                            P.mm(bank(bk), wsl[s][:, k, :], ain[a][:, k, sl], k == 0, k == KC - 1,
                                 [wsl_r[s], ain_r[a]], [ps_r[bk]])
                        z_chunk(z, z_r, m, sl, bk, xres[xs_], xres_r[xs_], s1, s2, sq, sq_r, st_r)
                ln_apply(z, z_r, s1, s2, s3, st_r, T, lnidx, t0, Xout, Xout_r, False, None, None, None)
            P.barrier()
            P.emit()

    def ffn_phase(l, Xin, Xout, lnidx, T, final):
        with ExitStack() as es:
            def sb(name, shape, dt=F32):
                return es.enter_context(nc.sbuf_tensor(un(name), list(shape), dt))
            xin = sb("f_xin", [128, KC, T], BF16); xin_r = Res()
            actT = sb("f_act", [128, JC, T], BF16); act_r = [Res() for _ in range(JC)]
            z = sb("f_z", [128, KC, T]); z_r = [Res() for _ in range(KC)]
            NG = 3
            wg = [sb(f"f_wg{i}", [128, KC, 256], BF16) for i in range(NG)]; wg_r = [(Res(), Res()) for _ in range(NG)]
            ND = 2
            wd = [sb(f"f_wd{i}", [128, JC, 128], BF16) for i in range(ND)]; wd_r = [Res() for _ in range(ND)]
            sg = [sb(f"f_sg{i}", [128, 512]) for i in range(2)]; sg_r = [Res(), Res()]
            xres = [sb(f"f_xr{i}", [128, T]) for i in range(2)]; xres_r = [Res(), Res()]
            s1 = sb("f_s1", [128, T]); s2 = sb("f_s2", [128, T]); s3 = sb("f_s3", [128, T])
            sq = sb("f_sq", [128, T]); sq_r = Res()
            st_r = Res()
            Xout_r = Res()
            if final:
                ystage = [sb(f"f_ys{i}", [128, D]) for i in range(2)]
                ystage_r = [Res(), Res()]
            else:
                ystage = ystage_r = None
            Wg = w_gu[l].rearrange("(k p) m -> p k m", p=128)
            Wd = w_dn[l].rearrange("(j p) m -> p j m", p=128)
            ig = 0
            idn = 0
            isg = 0
            ix = 0
            for ti in range(NT // T):
                t0 = ti * T
                P.dma('pool', 'fx', xin[:], Xin[:, :, t0:t0 + T], writes=[xin_r])
                for j in range(JC):
                    s = ig % NG
                    ig += 1
                    P.dma('pool', ('fg', s), wg[s][:, :, 0:128], Wg[:, :, j * 128:(j + 1) * 128], writes=[wg_r[s][0]])
                    P.dma('pool', ('fu', s), wg[s][:, :, 128:256], Wg[:, :, DFF + j * 128:DFF + (j + 1) * 128],
                          writes=[wg_r[s][1]])
                    for hcol in range(T // 512):
                        sl = slice(hcol * 512, (hcol + 1) * 512)
                        bg = next_bank()
                        for k in range(KC):
                            P.mm(bank(bg), wg[s][:, k, 0:128], xin[:, k, sl], k == 0, k == KC - 1,
                                 [wg_r[s][0], xin_r], [ps_r[bg]])
                        bu = next_bank()
                        for k in range(KC):
                            P.mm(bank(bu), wg[s][:, k, 128:256], xin[:, k, sl], k == 0, k == KC - 1,
                                 [wg_r[s][1], xin_r], [ps_r[bu]])
                        q = isg % 2
                        isg += 1
                        P.act(sg[q][:], bank(bg), AF.Silu, [ps_r[bg]], [sg_r[q]])
                        P.tt('dve', actT[:, j, sl], sg[q][:], bank(bu), ALU.mult, [sg_r[q], ps_r[bu]], [act_r[j]])
                for m in range(KC):
                    s = idn % ND
                    idn += 1
                    P.dma('pool', ('fd', s), wd[s][:], Wd[:, :, m * 128:(m + 1) * 128], writes=[wd_r[s]])
                    xs_ = ix % 2
                    ix += 1
                    P.dma('sp', ('fr', xs_), xres[xs_][:], Xin[:, m, t0:t0 + T], writes=[xres_r[xs_]])
                    for hcol in range(T // 512):
                        sl = slice(hcol * 512, (hcol + 1) * 512)
                        bk = next_bank()
                        for j in range(JC):
                            P.mm(bank(bk), wd[s][:, j, :], actT[:, j, sl], j == 0, j == JC - 1,
                                 [wd_r[s], act_r[j]], [ps_r[bk]])
                        z_chunk(z, z_r, m, sl, bk, xres[xs_], xres_r[xs_], s1, s2, sq, sq_r, st_r)
                ln_apply(z, z_r, s1, s2, s3, st_r, T, lnidx, t0, Xout, Xout_r, final, ystage, ystage_r, 'fy')
            P.barrier()
            P.emit()

    if stage >= 1:
        linear_ln_phase(w_lo, X0, X1, 0, T_LIN)
    if stage >= 2:
        ffn_phase(0, X1, X2, 1, T_FFN, False)

    if stage >= 3:
        with ExitStack() as es0:
            def sb0(name, shape, dt=F32):
                return es0.enter_context(nc.sbuf_tensor(un(name), list(shape), dt))
            qT = sb0("a_qT", [128, 16, NT], BF16); qT_r = [Res() for _ in range(16)]
            kT = sb0("a_kT", [128, 4, 18 * 128], BF16); kT_r = [Res() for _ in range(4)]
            vv = sb0("a_v", [128, 18, 512], BF16); vv_r = [Res() for _ in range(18)]
            es1 = ExitStack()
            xh = es1.enter_context(nc.sbuf_tensor(un("a_xh"), [128, KC, 256], BF16)); xh_r = Res()
            wv = es1.enter_context(nc.sbuf_tensor(un("a_wv"), [128, KC, 512], BF16)); wv_r = Res()
            Wq = w_qkv.rearrange("(k p) m -> p k m", p=128)
            agr = Res()
            ag2v = ag2_in.rearrange("p (k t) -> p k t", k=KC)
            P.dma('sp', 'ag2', ag2v[:, :, 0:128], X2[:, :, 0:128], writes=[agr])
            P.dma('sp', 'ag2', ag2v[:, :, 128:256], X2[:, :, NT - 128:NT], writes=[agr])
            P.coll('ag2c', lambda e: e.collective_compute("AllGather", ALU.bypass, replica_groups=rg,
                                                          ins=[ag2_in.opt()], outs=[ag2_out.opt()]),
                   reads=[agr], writes=[agr])
            with ExitStack() as es:
                def sb(name, shape, dt=F32):
                    return es.enter_context(nc.sbuf_tensor(un(name), list(shape), dt))
                xin = sb("q_xin", [128, KC, NT], BF16); xin_r = Res()
                NWS = 3
                wsl = [sb(f"q_w{i}", [128, KC, 128], BF16) for i in range(NWS)]; wsl_r = [Res() for _ in range(NWS)]
                for t in range(4):
                    P.dma('pool', 'qx', xin[:, :, t * 512:(t + 1) * 512], X2[:, :, t * 512:(t + 1) * 512], writes=[xin_r])
                P.dma('pool', 'qv', wv[:], Wq[:, :, 2560:3072], writes=[wv_r])
                it = 0
                for m in range(20):
                    s = it % NWS
                    it += 1
                    P.dma('pool', ('qw', s), wsl[s][:], Wq[:, :, m * 128:(m + 1) * 128], writes=[wsl_r[s]])
                    for t in range(4):
                        bk = next_bank()
                        for k in range(KC):
                            P.mm(bank(bk), wsl[s][:, k, :], xin[:, k, t * 512:(t + 1) * 512], k == 0, k == KC - 1,
                                 [wsl_r[s], xin_r], [ps_r[bk]])
                        if m < 16:
                            dst, dr_ = qT[:, m, t * 512:(t + 1) * 512], qT_r[m]
                        else:
                            dst, dr_ = kT[:, m - 16, 128 + t * 512:128 + (t + 1) * 512], kT_r[m - 16]
                        if t % 2 == 0:
                            P.act(dst, bank(bk), AF.Copy, [ps_r[bk]], [dr_])
                        else:
                            P.cp('dve', dst, bank(bk), [ps_r[bk]], [dr_])
                for pos in range(1, 17):
                    c0 = (pos - 1) * 128
                    bk = next_bank()
                    for k in range(KC):
                        P.mm(bank(bk), xin[:, k, c0:c0 + 128], wv[:, k, :], k == 0, k == KC - 1, [xin_r, wv_r], [ps_r[bk]])
                    if pos % 2 == 0:
                        P.act(vv[:, pos, :], bank(bk), AF.Copy, [ps_r[bk]], [vv_r[pos]])
                    else:
                        P.cp('dve', vv[:, pos, :], bank(bk), [ps_r[bk]], [vv_r[pos]])
                P.barrier()
                P.emit()
            with ExitStack() as es:
                def sb(name, shape, dt=F32):
                    return es.enter_context(nc.sbuf_tensor(un(name), list(shape), dt))
                cand = [sb(f"q_cd{i}", [128, KC, 128]) for i in range(2)]; cand_r = [Res(), Res()]
                hacc = sb("q_hacc", [128, KC, 256]); hacc_r = Res()
                wk = [sb(f"q_wk{i}", [128, KC, 128], BF16) for i in range(4)]; wk_r = [Res() for _ in range(4)]
                for j in range(4):
                    P.dma('pool', ('qk', j), wk[j][:], Wq[:, :, (16 + j) * 128:(17 + j) * 128], writes=[wk_r[j]])
                P.op('dve', lambda e: e.memset(hacc[:], 0.0), writes=[hacc_r])
                ic = 0
                for r in range(NCORES):
                    gv = ag2_out[r * 128:(r + 1) * 128, :].rearrange("p (k t) -> p k t", k=KC)
                    for side in range(2):
                        c = ic % 2
                        ic += 1
                        srcv = gv[:, :, 128:256] if side == 0 else gv[:, :, 0:128]
                        P.dma('sp', ('qc', c), cand[c][:], srcv, reads=[agr], writes=[cand_r[c]])
                        sel = vcol((V_SELL if side == 0 else V_SELR) + r)
                        dsth = hacc[:, :, side * 128:(side + 1) * 128]
                        P.stt('dve', dsth, cand[c][:], sel, dsth, ALU.mult, ALU.add, [cand_r[c], consts_r, hacc_r], [hacc_r])
                P.cp('dve', xh[:], hacc[:], [hacc_r], [xh_r])
                for j in range(4):
                    bk = next_bank()
                    for k in range(KC):
                        P.mm(bank(bk)[:, 0:256], wk[j][:, k, :], xh[:, k, :], k == 0, k == KC - 1,
                             [wk_r[j], xh_r], [ps_r[bk]])
                    P.act(kT[:, j, 0:128], bank(bk)[:, 0:128], AF.Copy, [ps_r[bk]], [kT_r[j]])
                    P.act(kT[:, j, 17 * 128:18 * 128], bank(bk)[:, 128:256], AF.Copy, [ps_r[bk]], [kT_r[j]])
                for pos, c0 in ((0, 0), (17, 128)):
                    bk = next_bank()
                    for k in range(KC):
                        P.mm(bank(bk), xh[:, k, c0:c0 + 128], wv[:, k, :], k == 0, k == KC - 1, [xh_r, wv_r], [ps_r[bk]])
                    P.cp('dve', vv[:, pos, :], bank(bk), [ps_r[bk]], [vv_r[pos]])
                P.barrier()
                P.emit()
            es1.close()
            with ExitStack() as es:
                def sb(name, shape, dt=F32):
                    return es.enter_context(nc.sbuf_tensor(un(name), list(shape), dt))
                biasb = sb("c_bias", [128, 16, 384]); biasb_r = Res()
                band = sb("c_band", [128, 384]); band_r = Res()
                S2 = [sb(f"c_S{i}", [128, 4, 384]) for i in range(2)]; S2_r = [Res(), Res()]
                Pn2 = [sb(f"c_Pn{i}", [128, 4, 384], BF16) for i in range(2)]; Pn2_r = [Res(), Res()]
                PT2 = [sb(f"c_PT{i}", [128, 3, 512], BF16) for i in range(2)]; PT2_r = [Res(), Res()]
                st42 = [sb(f"c_st{i}", [128, 32]) for i in range(2)]; st42_r = [Res(), Res()]
                Ost = [sb(f"c_O{i}", [128, 4, NT], BF16) for i in range(2)]; Ost_r = [Res(), Res()]
                P.dma('sp', 'cb', biasb[:], bias_in[:], writes=[biasb_r])
                P.dma('sp', 'cb', band[:], band_in[:], writes=[band_r])
                for h in range(16):
                    P.tt('dve', biasb[:, h, :], biasb[:, h, :], band[:], ALU.add, [biasb_r, band_r], [biasb_r])
                scale = 128.0 ** -0.5
                psA3 = psA[:, :].rearrange("p (h c) -> p h c", h=4)
                ptps = psB[:, 0:1024].bitcast(BF16)
                itn = 0
                for g in range(4):
                    og = g % 2
                    for n in range(16):
                        par = itn % 2
                        itn += 1
                        S = S2[par]; S_r = S2_r[par]; Pn = Pn2[par]; Pn_r = Pn2_r[par]
                        PT = PT2[par]; PT_r = PT2_r[par]; st4 = st42[par]; st4_r = st42_r[par]
                        ob = 6 + par
                        mx = st4[:, 0:4]; mm_ = st4[:, 4:8]; negm = st4[:, 8:12]; rs = st4[:, 12:16]
                        es_ = st4[:, 16:20]; den = st4[:, 20:24]; rinv = st4[:, 24:28]
                        sinkg = vecs[:, V_SINK + 4 * g:V_SINK + 4 * g + 4]
                        for h in range(4):
                            P.mm(psA[:, h * 512:h * 512 + 384], qT[:, 4 * g + h, n * 128:(n + 1) * 128],
                                 kT[:, g, n * 128:(n + 3) * 128], True, True, [qT_r[4 * g + h], kT_r[g]], [ps_r[h]])
                        P.stt('dve', S[:], psA3[:, :, 0:384], scale, biasb[:, 4 * g:4 * g + 4, :], ALU.mult, ALU.add,
                              [ps_r[0], ps_r[1], ps_r[2], ps_r[3], biasb_r], [S_r])
                        if n == 0:
                            P.ts('dve', S[:, :, 0:128], S[:, :, 0:128], vcol(V_HM), None, ALU.add, None, [S_r, consts_r], [S_r])
                        if n == 15:
                            P.ts('dve', S[:, :, 256:384], S[:, :, 256:384], vcol(V_HM + 1), None, ALU.add, None,
                                 [S_r, consts_r], [S_r])
                        P.op('dve', lambda e, mx=mx, S=S: e.tensor_reduce(out=mx, in_=S[:], axis=AX.X, op=ALU.max), [S_r], [st4_r])
                        P.tt('dve', mm_, mx, sinkg, ALU.max, [st4_r, consts_r], [st4_r])
                        P.ts('dve', negm, mm_, -1.0, None, ALU.mult, None, [st4_r], [st4_r])
                        for h in range(4):
                            P.act(S[:, h, :], S[:, h, :], AF.Exp, [S_r, st4_r], [S_r, st4_r], bias=negm[:, h:h + 1],
                                  accum_out=rs[:, h:h + 1])
                        P.tt('dve', es_, sinkg, negm, ALU.add, [st4_r, consts_r], [st4_r])
                        P.act(es_, es_, AF.Exp, [st4_r], [st4_r])
                        P.tt('dve', den, rs, es_, ALU.add, [st4_r], [st4_r])
                        P.op('dve', lambda e, rinv=rinv, den=den: e.reciprocal(out=rinv, in_=den), [st4_r], [st4_r])
                        for h in range(4):
                            P.ts('dve', Pn[:, h, :], S[:, h, :], rinv[:, h:h + 1], None, ALU.mult, None, [S_r, st4_r], [Pn_r])
                        for kb in range(3):
                            for h in range(4):
                                P.tr(ptps[:, kb * 512 + h * 128:kb * 512 + (h + 1) * 128], Pn[:, h, kb * 128:(kb + 1) * 128],
                                     identb[:], [Pn_r, consts_r], [ps_r[4], ps_r[5]])
                        P.act(PT[:].rearrange("p a b -> p (a b)"), ptps[:, 0:1536], AF.Copy, [ps_r[4], ps_r[5]], [PT_r])
                        for kb in range(3):
                            P.mm(bank(ob), vv[:, n + kb, g * 128:(g + 1) * 128], PT[:, kb, :], kb == 0, kb == 2,
                                 [vv_r[n + kb], PT_r], [ps_r[ob]])
                        P.cp('dve', Ost[og][:, :, n * 128:(n + 1) * 128], bank(ob).rearrange("p (h q) -> p h q", h=4),
                             [ps_r[ob]], [Ost_r[og]])
                    P.dma('sp', ('co', og), GT[:, 4 * g:4 * g + 4, :], Ost[og][:], reads=[Ost_r[og]])
                P.barrier()
                P.emit()
    if stage >= 4:
        linear_ln_phase(w_ao, X2, X3, 2, T_LIN)
    if stage >= 5:
        ffn_phase(1, X3, y_out, 3, T_FFN, True)
    if debug:
        for nm, X in (("dX1", X1), ("dX2", X2), ("dX3", X3)):
            dd = nc.dram_tensor(nm, [128, KC, 256], F32, kind="ExternalOutput").ap()
            P.dma('sp', 'dbg', dd[:, :, 0:128], X[:, :, 0:128])
            P.dma('sp', 'dbg', dd[:, :, 128:256], X[:, :, NT - 128:NT])
        P.barrier()
        P.emit()
    return nc


_T5_CACHE = {}


def _t5_bucket(rel):
    nb = 16
    ret = (rel > 0).astype(np.int32) * nb
    n = np.abs(rel)
    max_exact = nb // 2
    nn = np.maximum(n, 1).astype(np.float32)
    large = max_exact + (np.log(nn / max_exact) / math.log(128 / max_exact) * (nb - max_exact)).astype(np.int32)
    large = np.minimum(large, nb - 1)
    return (ret + np.where(n < max_exact, n, large)).astype(np.int32)


def _pcol(v):
    v = np.asarray(v, np.float32).reshape(-1, 16, 128)
    return np.ascontiguousarray(v.transpose(2, 0, 1).reshape(128, -1))


def make_in_maps(inp):
    f = np.float32
    xp = np.asarray(inp["x_prompt"], f)[0]
    xs = np.asarray(inp["x_sample"], f)
    q = np.arange(128)[:, None]
    c = np.arange(384)[None, :]
    rel = c - 128 - q
    bucket = _t5_bucket(rel)
    band = np.where(np.abs(rel) <= 128, 0.0, NEG).astype(f)
    rb = np.asarray(inp["rel_bias"], f)
    attn_bias = np.ascontiguousarray(rb[bucket].transpose(0, 2, 1))
    shared = {
        "ident": np.eye(128, dtype=f),
        "attn_bias": attn_bias,
        "band": band,
        "lru_w_in": np.ascontiguousarray(np.asarray(inp["lru_w_in"], f)[0]),
        "lru_gate_w": np.ascontiguousarray(np.asarray(inp["lru_gate_w"], f)[0]),
        "lru_w_out": np.ascontiguousarray(np.asarray(inp["lru_w_out"], f)[0]),
        "attn_w_qkv": np.ascontiguousarray(np.asarray(inp["attn_w_qkv"], f)[0]),
        "attn_w_out": np.ascontiguousarray(np.asarray(inp["attn_w_out"], f)[0]),
        "ffn_w_gu": np.ascontiguousarray(np.asarray(inp["ffn_w_gu"], f)),
        "ffn_w_down": np.ascontiguousarray(np.asarray(inp["ffn_w_down"], f)),
    }
    vbase = np.zeros((128, NV), f)
    vbase[:, V_CW:V_CW + 64] = _pcol(np.asarray(inp["lru_conv_w"], f)[0])
    vbase[:, V_CB:V_CB + 16] = _pcol(np.asarray(inp["lru_conv_b"], f)[0])
    gb = np.asarray(inp["lru_gate_b"], f)[0]
    vbase[:, V_GB:V_GB + 64] = np.ascontiguousarray(gb.transpose(3, 0, 1, 2).reshape(128, 64))
    vbase[:, V_LAM:V_LAM + 32] = _pcol(np.asarray(inp["lru_lambda"], f)[0])
    vbase[:, V_LNG:V_LNG + 64] = _pcol(np.asarray(inp["ln_g"], f).reshape(4, D))
    vbase[:, V_LNB:V_LNB + 64] = _pcol(np.asarray(inp["ln_b"], f).reshape(4, D))
    vbase[:, V_SINK:V_SINK + 16] = np.asarray(inp["attn_sink"], f)[0][None, :]
    maps = []
    for core in range(NCORES):
        xc = np.zeros((NTH, D), f)
        v = vbase.copy()
        if core < 4:
            xc[:NT] = xp[core * NT:(core + 1) * NT]
            if core > 0:
                xc[NT:NT + 2] = xp[core * NT - 2:core * NT]
                v[:, V_SELL + core - 1] = 1.0
            else:
                v[:, V_HM] = NEG
            if core < 3:
                xc[NT + 2] = xp[(core + 1) * NT]
                v[:, V_SELR + core + 1] = 1.0
            else:
                v[:, V_HM + 1] = NEG
            for j in range(4):
                if j < core:
                    v[:, V_SELF + j] = 1.0
                if j > core:
                    v[:, V_SELB + j] = 1.0
        else:
            xc[:NT] = xs[core - 4]
            v[:, V_HM] = NEG
            v[:, V_HM + 1] = NEG
        m = dict(shared)
        m["x"] = xc
        m["vecs"] = v
        maps.append(m)
    return maps


_NC_CACHE = {}


def kernel(**inputs):
    if "nc" not in _NC_CACHE:
        _NC_CACHE["nc"] = build_program()
    nc = _NC_CACHE["nc"]
    maps = make_in_maps(inputs)
    res = run_bass_kernel_spmd(nc, maps, core_ids=list(range(NCORES)))
    ys = [np.asarray(r["y"], np.float32) for r in res.results]
    y_prompt = np.concatenate(ys[0:4], axis=0)[None]
    y_sample = np.stack(ys[4:8], axis=0)
    return (y_prompt, y_sample)
```
